# Optimizing a Trainium2 kernel written in Bass

```python
import math
import jax, jax.numpy as jnp
from jax import lax
import numpy as np

D_MODEL = 1024
BATCH = 2
SEQ = 8192
DEPTH = 2

N_EVEN = (DEPTH + 1) // 2
N_ODD = DEPTH // 2
RMS_EPS = 1e-6
LN_EPS = 1e-5

W_A = D_MODEL
CONV_K = 31
A_GROUPS = 8
H_B = 4
DQK_B = D_MODEL // (2 * H_B)
DV_B = D_MODEL // H_B
W_B = H_B * DV_B
CHUNK = 64
H_C = 8
QK_NOPE = 128
QK_ROPE = 64
V_HEAD = 128
Q_LORA = 3 * D_MODEL // 8
KV_LORA = D_MODEL // 4
W_C = H_C * V_HEAD
Q_BLOCK = 128
ROPE_THETA = 10000.0
D_GROUPS = 4
D_GROUP_CH = D_MODEL // 8
W_D = D_GROUPS * D_GROUP_CH

EVEN_SIZES = (W_A, W_A, W_A, H_B * DQK_B, H_B * DQK_B, W_B, W_B, W_B, 4 * H_B)
ODD_SIZES = (Q_LORA, KV_LORA, QK_ROPE, W_D, W_C, W_D)
P_EVEN = sum(EVEN_SIZES)
P_ODD = sum(ODD_SIZES)

kernel_name = "hybrid_conv_mlstm_mla_fnet_encoder"


def split_cols(p, sizes):
    idx = [int(i) for i in np.cumsum(sizes)[:-1]]
    return jnp.split(p, idx, axis=-1)


def rms_norm(x, g):
    xf = x.astype(jnp.float32)
    y = xf * lax.rsqrt(jnp.mean(xf * xf, axis=-1, keepdims=True) + RMS_EPS)
    return (y * g.astype(jnp.float32)).astype(x.dtype)


def group_layer_norm(x, g, b, groups):
    B, S, C = x.shape
    xf = x.astype(jnp.float32).reshape(B, S, groups, C // groups)
    mu = jnp.mean(xf, axis=-1, keepdims=True)
    xc = xf - mu
    var = jnp.mean(xc * xc, axis=-1, keepdims=True)
    y = (xc * lax.rsqrt(var + LN_EPS)).reshape(B, S, C)
    return (y * g.astype(jnp.float32) + b.astype(jnp.float32)).astype(x.dtype)


def depthwise_conv_centred(x, w, b):
    K = w.shape[0]
    y = lax.conv_general_dilated(
        x, w[:, None, :].astype(x.dtype), window_strides=(1,),
        padding=[(K // 2, K // 2)], dimension_numbers=("NWC", "WIO", "NWC"),
        feature_group_count=x.shape[-1])
    return y + b.astype(x.dtype)


def rope_tables(positions):
    inv = ROPE_THETA ** (-jnp.arange(0, QK_ROPE, 2, dtype=jnp.float32) / QK_ROPE)
    ang = positions.astype(jnp.float32)[..., None] * inv
    return jnp.cos(ang), jnp.sin(ang)


def apply_rope(x, cos, sin):
    half = x.shape[-1] // 2
    xf = x.astype(jnp.float32)
    x1, x2 = xf[..., :half], xf[..., half:]
    return jnp.concatenate([x1 * cos - x2 * sin, x1 * sin + x2 * cos], axis=-1).astype(x.dtype)


def mlstm_chunkwise(q, k, v, i_pre, f_pre):
    B, H, S, dk = q.shape
    dv = v.shape[-1]
    nc = S // CHUNK
    f32 = jnp.float32

    def chunks(a):
        return jnp.moveaxis(a.reshape(B, H, nc, CHUNK, *a.shape[3:]), 2, 0)

    qc = chunks(q.astype(f32))
    kc = chunks(k.astype(f32) * (dk ** -0.5))
    vc = chunks(v.astype(f32))
    ic = chunks(i_pre.astype(f32))
    bc = jnp.cumsum(chunks(jax.nn.log_sigmoid(f_pre.astype(f32))), axis=-1)
    tril = jnp.tril(jnp.ones((CHUNK, CHUNK), dtype=bool))

    def step(carry, inp):
        C, n, m = carry
        qb, kb, vb, ib, bb = inp
        D = jnp.where(tril, bb[..., :, None] - bb[..., None, :] + ib[..., None, :], -jnp.inf)
        a = bb + m[..., None]
        m_t = jnp.maximum(a, jnp.max(D, axis=-1))
        Dw = jnp.exp(D - m_t[..., None])
        aw = jnp.exp(a - m_t)
        s = jnp.einsum('bhtd,bhsd->bhts', qb, kb) * Dw
        num = jnp.einsum('bhts,bhsv->bhtv', s, vb) + aw[..., None] * jnp.einsum('bhtd,bhdv->bhtv', qb, C)
        den = jnp.sum(s, axis=-1) + aw * jnp.einsum('bhtd,bhd->bht', qb, n)
        h = num / jnp.maximum(jnp.abs(den), jnp.exp(-m_t))[..., None]
        bL = bb[..., -1]
        g = bL[..., None] - bb + ib
        m_new = jnp.maximum(bL + m, jnp.max(g, axis=-1))
        wk = jnp.exp(g - m_new[..., None])
        decay = jnp.exp(bL + m - m_new)
        C_new = decay[..., None, None] * C + jnp.einsum('bhs,bhsd,bhsv->bhdv', wk, kb, vb)
        n_new = decay[..., None] * n + jnp.einsum('bhs,bhsd->bhd', wk, kb)
        return (C_new, n_new, m_new), h

    init = (jnp.zeros((B, H, dk, dv), f32), jnp.zeros((B, H, dk), f32), jnp.zeros((B, H), f32))
    _, hs = lax.scan(step, init, (qc, kc, vc, ic, bc))
    return jnp.moveaxis(hs, 0, 2).reshape(B, H, S, dv)


def even_layer(x, g_pre, g_post, w_in, b_gate, conv_w, conv_b, gn_g, gn_b, hn_g, w_out):
    B, S, _ = x.shape
    h = rms_norm(x, g_pre)
    p = h @ w_in
    a_val, a_gate, z_a, q, k, v, o, z_b, gates = split_cols(p, EVEN_SIZES)
    u = a_val * jax.nn.sigmoid(a_gate)
    u = depthwise_conv_centred(u, conv_w, conv_b)
    u = jax.nn.silu(group_layer_norm(u, gn_g, gn_b, A_GROUPS))
    y_a = u * jax.nn.silu(z_a)
    to_heads = lambda t, d: t.reshape(B, S, H_B, d).transpose(0, 2, 1, 3)
    qh, kh, vh = to_heads(q, DQK_B), to_heads(k, DQK_B), to_heads(v, DV_B)
    gt = (gates.astype(jnp.float32) + b_gate.astype(jnp.float32)).reshape(B, S, 4, H_B).transpose(2, 0, 3, 1)
    i_f, f_f, i_b, f_b = gt[0], gt[1], gt[2], gt[3]
    h_fwd = mlstm_chunkwise(qh, kh, vh, i_f, f_f)
    fl = lambda t: jnp.flip(t, axis=2)
    h_bwd = fl(mlstm_chunkwise(fl(qh), fl(kh), fl(vh), fl(i_b), fl(f_b)))
    hm = (h_fwd + h_bwd).transpose(0, 2, 1, 3)
    hm = rms_norm(hm, hn_g.reshape(H_B, DV_B)).astype(x.dtype)
    hm = jax.nn.sigmoid(o.reshape(B, S, H_B, DV_B)) * hm
    y_b = hm.reshape(B, S, W_B) * jax.nn.silu(z_b)
    y = jnp.concatenate([y_a, y_b], axis=-1) @ w_out
    return x + rms_norm(y, g_post)


def odd_layer(x, cos, sin, g_pre, g_post, w_in, g_q, w_uq, g_kv, w_ukv, w_fd, w_out):
    B, S, _ = x.shape
    h = rms_norm(x, g_pre)
    p = h @ w_in
    c_q, c_kv, k_r, f_in, z_c, z_d = split_cols(p, ODD_SIZES)
    qh = (rms_norm(c_q, g_q) @ w_uq).reshape(B, S, H_C, QK_NOPE + QK_ROPE)
    q_nope = qh[..., :QK_NOPE]
    q_rope = apply_rope(qh[..., QK_NOPE:], cos[:, :, None, :], sin[:, :, None, :])
    kv = (rms_norm(c_kv, g_kv) @ w_ukv).reshape(B, S, H_C, QK_NOPE + V_HEAD)
    k_nope, v = kv[..., :QK_NOPE], kv[..., QK_NOPE:]
    k_rope = apply_rope(k_r, cos, sin)
    scale = (QK_NOPE + QK_ROPE) ** -0.5
    nb = S // Q_BLOCK
    blk = lambda t: jnp.moveaxis(t.reshape(B, nb, Q_BLOCK, *t.shape[2:]), 1, 0)

    def attend(qs):
        qn, qr = qs
        s = jnp.einsum('bqhd,bkhd->bhqk', qn, k_nope) + jnp.einsum('bqhr,bkr->bhqk', qr, k_rope)
        pr = jax.nn.softmax(s.astype(jnp.float32) * scale, axis=-1).astype(v.dtype)
        return jnp.einsum('bhqk,bkhd->bqhd', pr, v)

    att = lax.map(attend, (blk(q_nope), blk(q_rope)))
    att = jnp.moveaxis(att, 0, 1).reshape(B, S, W_C)
    y_c = att * jax.nn.silu(z_c)
    fg = f_in.astype(jnp.float32).reshape(B, S, D_GROUPS, D_GROUP_CH)
    fr = jnp.real(jnp.fft.fft2(fg, axes=(1, 3), norm="ortho")).astype(x.dtype).reshape(B, S, W_D)
    y_d = (fr @ w_fd) * jax.nn.silu(z_d)
    y = jnp.concatenate([y_c, y_d], axis=-1) @ w_out
    return x + rms_norm(y, g_post)


def setup_inputs(seed: int = 0) -> dict:
    key = jax.random.key(seed)
    ks = jax.random.split(key, 24)
    f32 = jnp.float32
    nrm = lambda k, shape, fan_in: jax.random.normal(k, shape, f32) * (fan_in ** -0.5)
    gain = lambda k, shape: 1.0 + 0.05 * jax.random.normal(k, shape, f32)
    small = lambda k, shape: 0.02 * jax.random.normal(k, shape, f32)
    x = jax.random.normal(ks[0], (BATCH, SEQ, D_MODEL), f32)
    positions = (jnp.arange(SEQ, dtype=jnp.int32)[None, :]
                 + jax.random.randint(ks[1], (BATCH, 1), 0, 1024, dtype=jnp.int32))
    fbias = jnp.linspace(3.0, 6.0, H_B, dtype=f32)
    gate_base = jnp.concatenate([jnp.zeros((H_B,), f32), fbias, jnp.zeros((H_B,), f32), fbias])
    even_b_gate = gate_base[None, :] + 0.1 * jax.random.normal(ks[2], (N_EVEN, 4 * H_B), f32)
    return {
        "x": x,
        "positions": positions,
        "even_g_pre": gain(ks[3], (N_EVEN, D_MODEL)),
        "even_g_post": gain(ks[4], (N_EVEN, D_MODEL)),
        "even_w_in": nrm(ks[5], (N_EVEN, D_MODEL, P_EVEN), D_MODEL),
        "even_b_gate": even_b_gate,
        "even_conv_w": nrm(ks[6], (N_EVEN, CONV_K, W_A), CONV_K),
        "even_conv_b": small(ks[7], (N_EVEN, W_A)),
        "even_gn_g": gain(ks[8], (N_EVEN, W_A)),
        "even_gn_b": small(ks[9], (N_EVEN, W_A)),
        "even_hn_g": gain(ks[10], (N_EVEN, W_B)),
        "even_w_out": nrm(ks[11], (N_EVEN, W_A + W_B, D_MODEL), W_A + W_B),
        "odd_g_pre": gain(ks[12], (N_ODD, D_MODEL)),
        "odd_g_post": gain(ks[13], (N_ODD, D_MODEL)),
        "odd_w_in": nrm(ks[14], (N_ODD, D_MODEL, P_ODD), D_MODEL),
        "odd_g_q": gain(ks[15], (N_ODD, Q_LORA)),
        "odd_w_uq": nrm(ks[16], (N_ODD, Q_LORA, H_C * (QK_NOPE + QK_ROPE)), Q_LORA),
        "odd_g_kv": gain(ks[17], (N_ODD, KV_LORA)),
        "odd_w_ukv": nrm(ks[18], (N_ODD, KV_LORA, H_C * (QK_NOPE + V_HEAD)), KV_LORA),
        "odd_w_fd": nrm(ks[19], (N_ODD, W_D, W_D), W_D),
        "odd_w_out": nrm(ks[20], (N_ODD, W_C + W_D, D_MODEL), W_C + W_D),
    }


def reference(x, positions, even_g_pre, even_g_post, even_w_in, even_b_gate, even_conv_w,
              even_conv_b, even_gn_g, even_gn_b, even_hn_g, even_w_out, odd_g_pre, odd_g_post,
              odd_w_in, odd_g_q, odd_w_uq, odd_g_kv, odd_w_ukv, odd_w_fd, odd_w_out):
    cos, sin = rope_tables(positions)
    h = x
    for layer in range(DEPTH):
        j = layer // 2
        if layer % 2 == 0:
            h = even_layer(h, even_g_pre[j], even_g_post[j], even_w_in[j], even_b_gate[j],
                           even_conv_w[j], even_conv_b[j], even_gn_g[j], even_gn_b[j],
                           even_hn_g[j], even_w_out[j])
        else:
            h = odd_layer(h, cos, sin, odd_g_pre[j], odd_g_post[j], odd_w_in[j], odd_g_q[j],
                          odd_w_uq[j], odd_g_kv[j], odd_w_ukv[j], odd_w_fd[j], odd_w_out[j])
    return h
```

```python
import numpy as np
from concourse.bass_utils import run_bass_kernel_spmd
from contextlib import ExitStack, contextmanager
import concourse.bass as bass
import concourse.mybir as mybir

F32 = mybir.dt.float32
BF16 = mybir.dt.bfloat16
I32 = mybir.dt.int32
AF = mybir.ActivationFunctionType
ALU = mybir.AluOpType
AX = mybir.AxisListType


class Buf:
    __slots__ = ("w", "r")

    def __init__(self):
        self.w = None
        self.r = []


class T:
    def __init__(self, t):
        self.t = t
        self.b = Buf()

    def __getitem__(self, k):
        return self.t[k]


class Op:
    __slots__ = ("eng", "fn", "deps", "pos", "inc", "cnt", "dsem", "dval", "is_dma")


ENGS = ("pe", "act", "dve", "pool", "sp")


class Prog:
    def __init__(self, nc, ndma=None):
        self.nc = nc
        self.ops = {e: [] for e in ENGS}
        self.seen = {e: {} for e in ENGS}
        self.stack = ExitStack()
        ndma = ndma or {"sp": 20, "act": 6, "pool": 10}
        self.esem = {e: self.stack.enter_context(nc.semaphore(f"e_{e}")) for e in ENGS if e != "sp"}
        self.dsems = {}
        self.dcount = {}
        self.dlast = {}
        self.drr = {}
        for q, n in ndma.items():
            self.dsems[q] = [self.stack.enter_context(nc.semaphore(f"d_{q}{i}")) for i in range(n)]
            self.drr[q] = 0
        self.nalloc = 0
        self.final_ops = []

    def sb(self, shape, dtype, name=None):
        self.nalloc += 1
        t = self.stack.enter_context(self.nc.sbuf_tensor(name or f"sb{self.nalloc}", list(shape), dtype))
        return T(t)

    def ps(self, shape, dtype=F32, name=None):
        self.nalloc += 1
        t = self.stack.enter_context(self.nc.psum_tensor(name or f"ps{self.nalloc}", list(shape), dtype))
        return T(t)

    @staticmethod
    def _bufs(xs):
        out = []
        for x in xs:
            if x is None:
                continue
            out.append(x.b if hasattr(x, "b") else x)
        return out

    def emit(self, eng, fn, R=(), W=(), dma=False, final=False, extra=()):
        op = Op()
        op.eng = eng
        op.fn = fn
        op.is_dma = dma
        op.inc = False
        op.cnt = None
        op.dsem = None
        op.dval = None
        lst = self.ops[eng]
        op.pos = len(lst)
        deps = list(extra)
        Rb = self._bufs(R)
        Wb = self._bufs(W)
        for b in Rb:
            if b.w is not None:
                deps.append(b.w)
        for b in Wb:
            if b.w is not None:
                deps.append(b.w)
            deps.extend(b.r)
        if dma:
            q = eng
            k = self.drr[q]
            self.drr[q] = (k + 1) % len(self.dsems[q])
            key = (q, k)
            prev = self.dlast.get(key)
            if prev is not None:
                deps.append(prev)
            self.dcount[key] = self.dcount.get(key, 0) + 1
            op.dsem = key
            op.dval = 16 * self.dcount[key]
            self.dlast[key] = op
        seen = self.seen[eng]
        best = {}
        for d in deps:
            if d is op:
                continue
            if d.is_dma:
                k2 = ("d",) + d.dsem
                if seen.get(k2, 0) >= d.dval:
                    continue
                if k2 not in best or best[k2].dval < d.dval:
                    best[k2] = d
            else:
                if d.eng == eng and eng == "pe":
                    continue
                k2 = d.eng
                if seen.get(k2, -1) >= d.pos:
                    continue
                if k2 not in best or best[k2].pos < d.pos:
                    best[k2] = d
        op.deps = list(best.values())
        for k2, d in best.items():
            if d.is_dma:
                seen[k2] = d.dval
            else:
                seen[k2] = d.pos
                d.inc = True
        for b in Rb:
            b.r.append(op)
        for b in Wb:
            b.w = op
            b.r = []
        lst.append(op)
        if final:
            self.final_ops.append(op)
        return op

    def barrier(self):
        lasts = []
        for e in ENGS:
            for op in reversed(self.ops[e]):
                if (not op.is_dma) and op.fn is not None:
                    lasts.append(op)
                    break
        lasts += list(self.dlast.values())
        for e in ENGS:
            self.emit(e, None, extra=lasts)

    @contextmanager
    def scope(self):
        saved = self.stack
        with ExitStack() as es:
            self.stack = es
            yield
            self.stack = saved
        self.barrier()

    def mm(self, out, lhsT, rhs, start=True, stop=True, R=(), W=(), **kw):
        return self.emit("pe", lambda e: e.matmul(out, lhsT, rhs, start=start, stop=stop, **kw), R, W)

    def transpose(self, out, in_, ident, R=(), W=()):
        return self.emit("pe", lambda e: e.transpose(out, in_, ident), R, W)

    def act(self, out, in_, func, R=(), W=(), **kw):
        return self.emit("act", lambda e: e.activation(out, in_, func, **kw), R, W)

    def v(self, eng, name, *args, R=(), W=(), **kw):
        return self.emit(eng, lambda e: getattr(e, name)(*args, **kw), R, W)

    def dma(self, q, out, in_, R=(), W=(), final=False, **kw):
        return self.emit(q, lambda e: e.dma_start(out=out, in_=in_, **kw), R, W, dma=True, final=final)

    def finish(self, final_wait=()):
        nc = self.nc
        final_wait = list(final_wait) + self.final_ops
        if final_wait:
            self.emit("sp", None, R=(), W=())
            term = self.ops["sp"][-1]
            best = {}
            for d in final_wait:
                k2 = ("d",) + d.dsem
                if k2 not in best or best[k2].dval < d.dval:
                    best[k2] = d
            term.deps = list(best.values())
        for e in ENGS:
            c = 0
            for op in self.ops[e]:
                if op.is_dma:
                    continue
                if op.inc:
                    c += 1
                    op.cnt = c

        def run(e_name, eobj):
            for op in self.ops[e_name]:
                for d in op.deps:
                    if d.is_dma:
                        q, k = d.dsem
                        eobj.wait_ge(self.dsems[q][k], d.dval)
                    else:
                        eobj.wait_ge(self.esem[d.eng], d.cnt)
                if op.fn is None:
                    continue
                ins = op.fn(eobj)
                if op.is_dma:
                    q, k = op.dsem
                    ins.then_inc(self.dsems[q][k], 16)
                elif op.inc:
                    ins.then_inc(self.esem[e_name], 1)

        with nc.Block() as block:
            @block.sync
            def _(e):
                run("sp", e)

            @block.tensor
            def _(e):
                run("pe", e)

            @block.scalar
            def _(e):
                run("act", e)

            @block.vector
            def _(e):
                run("dve", e)

            @block.gpsimd
            def _(e):
                run("pool", e)
        self.stack.close()

RMS_EPS = 1e-6
LN_EPS = 1e-5
S = 8192
NCH = 64
DK_SCALE_LN = float(np.log(128.0 ** -0.5))


def dram_in(nc, name, shape, dt=F32):
    return nc.dram_tensor(name, list(shape), dt, kind="ExternalInput").ap()


def dram_out(nc, name, shape, dt=F32):
    return nc.dram_tensor(name, list(shape), dt, kind="ExternalOutput").ap()


def dram_tmp(nc, name, shape, dt=F32):
    return nc.dram_tensor(name, list(shape), dt, kind="Internal").ap()


def load_weight_bf16(P, W, wd, ncols, gscale=None, KC=8, piece=512, stg=None, col0=0, engs=("pool", "dve")):
    wv = wd.rearrange("(kc p) c -> p kc c", p=128)
    c0 = 0
    i = 0
    while c0 < ncols:
        c1 = min(ncols, c0 + piece)
        st = stg[i % len(stg)]
        P.dma("sp", st[:, :KC, : c1 - c0], wv[:, :, c0:c1], W=[st])
        for kc in range(KC):
            eng = engs[kc % len(engs)]
            if gscale is not None:
                P.v(eng, "tensor_scalar", W[:, kc, col0 + c0:col0 + c1], st[:, kc, : c1 - c0],
                    gscale[:, kc:kc + 1], None, ALU.mult, R=[st, gscale], W=[W])
            else:
                P.v(eng, "tensor_copy", W[:, kc, col0 + c0:col0 + c1], st[:, kc, : c1 - c0], R=[st], W=[W])
        c0 = c1
        i += 1


def rms_proj_pass(P, xT, W, n_fm, fm_col0, tm_groups, evac_fm, evac_tm, ones_bf, ntiles, D=1024, TT=512, pre=None):
    KC = D // 128
    xv = xT.rearrange("(kc p) t -> p kc t", p=128)
    xin = [P.sb([128, KC, TT], F32) for _ in range(2)]
    sq = P.sb([128, KC, TT], BF16)
    hT = [P.sb([128, KC, TT], BF16) for _ in range(2)]
    rstd = P.sb([128, TT], F32)
    ss_ps = P.ps([128, 512])
    fm_ps = [P.ps([128, 512]) for _ in range(2)]
    tm_ps = [[P.ps([128, 512]) for _ in range(2)] for _ in tm_groups]
    nfm = 0
    ntm = [0] * len(tm_groups)
    for ti in range(ntiles):
        t0 = ti * TT
        x = xin[ti % 2]
        h = hT[ti % 2]
        P.dma("sp", x[:], xv[:, :, t0:t0 + TT], W=[x])
        P.act(sq[:], x[:], AF.Square, R=[x], W=[sq])
        for kc in range(KC):
            P.mm(ss_ps[:], ones_bf[:], sq[:, kc, :], start=(kc == 0), stop=(kc == KC - 1), R=[sq, ones_bf], W=[ss_ps])
        P.act(rstd[:], ss_ps[:], AF.Sqrt, scale=1.0 / D, bias=RMS_EPS, R=[ss_ps], W=[rstd])
        P.v("dve", "reciprocal", rstd[:], rstd[:], R=[rstd], W=[rstd])
        for kc in range(KC):
            eng = "dve" if kc % 2 == 0 else "pool"
            P.v(eng, "tensor_tensor", h[:, kc, :], x[:, kc, :], rstd[:], ALU.mult, R=[x, rstd], W=[h])
        if pre is not None:
            pre(ti, x, h, rstd)
        for fc in range(n_fm):
            ps = fm_ps[nfm % 2]
            nfm += 1
            c0 = fm_col0 + fc * 128
            for kc in range(KC):
                P.mm(ps[:], W[:, kc, c0:c0 + 128], h[:, kc, :], start=(kc == 0), stop=(kc == KC - 1), R=[W, h], W=[ps])
            evac_fm(fc, ps, ti, t0)
        for sub in range(TT // 128):
            for gi, (c0, ncol) in enumerate(tm_groups):
                ps = tm_ps[gi][ntm[gi] % 2]
                ntm[gi] += 1
                for kc in range(KC):
                    P.mm(ps[:, :ncol], h[:, kc, sub * 128:(sub + 1) * 128], W[:, kc, c0:c0 + ncol],
                         start=(kc == 0), stop=(kc == KC - 1), R=[W, h], W=[ps])
                evac_tm(gi, ps, ti * (TT // 128) + sub)


def build_p1(nc, ntiles=16):
    P = Prog(nc)
    xT = dram_in(nc, "xT", [1024, S])
    w1 = dram_in(nc, "w1", [1024, 1924])
    gpre_d = dram_in(nc, "gpre", [128, 8])
    bg_d = dram_in(nc, "bg", [128, 4])
    cpar_d = dram_in(nc, "cpar", [128, 2, 34])
    hng_d = dram_in(nc, "hng", [128, 256])
    cst_d = dram_in(nc, "cst", [128, 3, 128])
    yT = dram_out(nc, "yT", [512, S], BF16)
    gb_d = dram_out(nc, "gb_scr", [NCH, 128, 256], BF16)
    nchunks = ntiles * 4

    cst = P.sb([128, 3, 128], F32)
    P.dma("sp", cst[:], cst_d, W=[cst])
    gpre = P.sb([128, 8], F32)
    P.dma("sp", gpre[:], gpre_d, W=[gpre])
    bg = P.sb([128, 4], F32)
    P.dma("sp", bg[:], bg_d, W=[bg])
    cpar = P.sb([128, 2, 34], F32)
    P.dma("sp", cpar[:], cpar_d, W=[cpar])
    hng = P.sb([128, 256], F32)
    P.dma("sp", hng[:], hng_d, W=[hng])
    ones_bf = P.sb([128, 128], BF16)
    P.v("pool", "memset", ones_bf[:], 1.0, W=[ones_bf])
    ones_f = P.sb([128, 128], F32)
    P.v("pool", "memset", ones_f[:], 1.0, W=[ones_f])
    avg_f = P.sb([128, 128], F32)
    P.v("pool", "memset", avg_f[:], 1.0 / 128.0, W=[avg_f])
    ident_bf = P.sb([128, 128], BF16)
    P.v("pool", "tensor_copy", ident_bf[:], cst[:, 0, :], R=[cst], W=[ident_bf])
    ident, triU, triL = cst[:, 0, :], cst[:, 1, :], cst[:, 2, :]

    cx = dict(cst=cst, gpre=gpre, bg=bg, hng=hng, ones_bf=ones_bf, ones_f=ones_f,
              ident_bf=ident_bf, xT=xT, w1=w1, yT=yT, gb_d=gb_d, nchunks=nchunks, ntiles=ntiles)
    with P.scope():
        build_p1_conv(P, cx, cpar, avg_f)
    return P, cx


def build_p1_conv(P, cx, cpar, avg_f):
    cst, gpre, ones_bf, xT, w1, yT, ntiles = cx["cst"], cx["gpre"], cx["ones_bf"], cx["xT"], cx["w1"], cx["yT"], cx["ntiles"]
    ident = cst[:, 0, :]
    W1 = P.sb([128, 8, 768], BF16)
    with P.scope():
        stg = [P.sb([128, 8, 512], F32) for _ in range(2)]
        load_weight_bf16(P, W1, w1[:, 0:768], 768, gscale=gpre, stg=stg)
    u_pad = P.sb([128, 2, S + 32], BF16)
    sza = P.sb([128, 2, S], BF16)
    P.v("pool", "memset", u_pad[:, :, 0:16], 0.0, W=[u_pad])
    P.v("pool", "memset", u_pad[:, :, 15 + ntiles * 512: 15 + ntiles * 512 + 16], 0.0, W=[u_pad])
    sg = [P.sb([128, 512], F32) for _ in range(2)]

    def evac_fm1(fc, ps, ti, t0):
        if fc in (0, 2):
            s_ = sg[(fc // 2) % 2]
            P.act(s_[:], ps[:], AF.Sigmoid, R=[ps], W=[s_])
        elif fc in (1, 3):
            c = fc // 2
            s_ = sg[c % 2]
            P.v("dve", "tensor_tensor", u_pad[:, c, 15 + t0:15 + t0 + 512], ps[:], s_[:], ALU.mult,
                R=[ps, s_], W=[u_pad])
        else:
            c = fc - 4
            P.act(sza[:, c, t0:t0 + 512], ps[:], AF.Silu, R=[ps], W=[sza])

    with P.scope():
        rms_proj_pass(P, xT, W1, 6, 0, [], evac_fm1, None, ones_bf, ntiles)

    diag = P.sb([128, 2, 31, 128], BF16)
    for c in range(2):
        for k in range(31):
            eng = "dve" if (k % 2 == 0) else "pool"
            P.v(eng, "tensor_scalar", diag[:, c, k, :], ident, cpar[:, c, k:k + 1], None, ALU.mult,
                R=[cst, cpar], W=[diag])
    with P.scope():
        cps = [P.ps([128, 512]) for _ in range(2)]
        mps = [P.ps([128, 512]) for _ in range(2)]
        vps = [P.ps([128, 512]) for _ in range(2)]
        csb = [P.sb([128, 512], F32) for _ in range(2)]
        xc = [P.sb([128, 512], F32) for _ in range(2)]
        sq2 = [P.sb([128, 512], F32) for _ in range(2)]
        rs2 = [P.sb([128, 512], F32) for _ in range(2)]
        ya = [P.sb([128, 512], BF16) for _ in range(2)]
        n = 0
        for ti in range(ntiles):
            t0 = ti * 512
            for c in range(2):
                i2 = n % 2
                n += 1
                for k in range(31):
                    P.mm(cps[i2][:], diag[:, c, k, :], u_pad[:, c, t0 + k:t0 + k + 512], start=(k == 0), stop=(k == 30),
                         R=[diag, u_pad], W=[cps[i2]])
                P.act(csb[i2][:], cps[i2][:], AF.Identity, bias=cpar[:, c, 31:32], R=[cps[i2], cpar], W=[csb[i2]])
                P.mm(mps[i2][:], avg_f[:], csb[i2][:], R=[avg_f, csb[i2]], W=[mps[i2]])
                P.v("dve", "tensor_tensor", xc[i2][:], csb[i2][:], mps[i2][:], ALU.subtract, R=[csb[i2], mps[i2]], W=[xc[i2]])
                P.act(sq2[i2][:], xc[i2][:], AF.Square, R=[xc[i2]], W=[sq2[i2]])
                P.mm(vps[i2][:], avg_f[:], sq2[i2][:], R=[avg_f, sq2[i2]], W=[vps[i2]])
                P.act(rs2[i2][:], vps[i2][:], AF.Sqrt, bias=LN_EPS, R=[vps[i2]], W=[rs2[i2]])
                P.v("dve", "reciprocal", rs2[i2][:], rs2[i2][:], R=[rs2[i2]], W=[rs2[i2]])
                P.v("pool", "tensor_tensor", xc[i2][:], xc[i2][:], rs2[i2][:], ALU.mult, R=[xc[i2], rs2[i2]], W=[xc[i2]])
                P.act(sq2[i2][:], xc[i2][:], AF.Silu, scale=cpar[:, c, 32:33], bias=cpar[:, c, 33:34],
                      R=[xc[i2], cpar], W=[sq2[i2]])
                P.v("pool", "tensor_tensor", ya[i2][:], sq2[i2][:], sza[:, c, t0:t0 + 512], ALU.mult,
                    R=[sq2[i2], sza], W=[ya[i2]])
                P.dma("pool", yT[c * 128:(c + 1) * 128, t0:t0 + 512], ya[i2][:], R=[ya[i2]], final=True)


def build_p1_mlstm(P, cx):
    nc = P.nc

    qT = P.sb([128, S], BF16)
    kT = P.sb([128, S], BF16)
    k_tm = P.sb([128, NCH, 128], BF16)
    v1 = P.sb([128, NCH, 258], BF16)
    G = P.sb([128, NCH, 4], F32)
    P.v("pool", "memset", v1[:, :, 256:258], 1.0, W=[v1])
    with P.scope():
        build_p1_a2(P, cx, qT, kT, k_tm, v1, G)
    import os
    if not os.environ.get('SKIPC'):
        build_p1_c(P, cx, qT, kT, k_tm, v1, G)


def build_p1_a2(P, cx, qT, kT, k_tm, v1, G):
    gpre, bg, ones_bf, xT, w1, gb_d, ntiles = cx["gpre"], cx["bg"], cx["ones_bf"], cx["xT"], cx["w1"], cx["gb_d"], cx["ntiles"]
    W2 = P.sb([128, 8, 1156], BF16)
    with P.scope():
        stg = [P.sb([128, 8, 512], F32) for _ in range(2)]
        load_weight_bf16(P, W2, w1[:, 768:1924], 1156, gscale=gpre, stg=stg)
    sig = [P.sb([128, 256], F32) for _ in range(2)]
    sil = [P.sb([128, 256], F32) for _ in range(2)]
    gbt = [P.sb([128, 256], BF16) for _ in range(2)]

    def evac_fm2(fc, ps, ti, t0):
        dst = qT if fc == 0 else kT
        P.act(dst[:, t0:t0 + 512], ps[:], AF.Copy, R=[ps], W=[dst])

    cnt = [0]

    def evac_tm2(gi, ps, c):
        if gi == 0:
            P.v("dve", "tensor_copy", k_tm[:, c, :], ps[:, 0:128], R=[ps], W=[k_tm])
            P.v("dve", "tensor_copy", v1[:, c, 0:256], ps[:, 128:384], R=[ps], W=[v1])
            P.v("dve", "tensor_tensor", G[:, c, :], ps[:, 384:388], bg[:], ALU.add, R=[ps, bg], W=[G])
        else:
            i2 = cnt[0] % 2
            cnt[0] += 1
            P.act(sig[i2][:], ps[:, 0:256], AF.Sigmoid, R=[ps], W=[sig[i2]])
            P.act(sil[i2][:], ps[:, 256:512], AF.Silu, R=[ps], W=[sil[i2]])
            P.v("pool", "tensor_tensor", gbt[i2][:], sig[i2][:], sil[i2][:], ALU.mult, R=[sig[i2], sil[i2]], W=[gbt[i2]])
            P.dma("pool", gb_d[c], gbt[i2][:], R=[gbt[i2]], W=[cx["gb_buf"][c]])

    rms_proj_pass(P, xT, W2, 2, 0, [(256, 388), (644, 512)], evac_fm2, evac_tm2, ones_bf, ntiles)


def build_p1_c(P, cx, qT, kT, k_tm, v1, G):
    import os
    cst, hng, ones_f, ident_bf = cx["cst"], cx["hng"], cx["ones_f"], cx["ident_bf"]
    yT, gb_d = cx["yT"], cx["gb_d"]
    NC_ = cx["nchunks"]
    triU, triL = cst[:, 1, :], cst[:, 2, :]

    E = [P.sb([128, NCH], F32) for _ in range(2)]
    WD = [P.sb([128, NCH], F32) for _ in range(2)]
    Wt = [P.sb([128, NCH], F32) for _ in range(2)]
    Dc = [P.sb([128, NCH], F32) for _ in range(2)]
    NCsave = NC_
    NC_ = int(os.environ.get('NCPRE', NC_))
    with P.scope():
        gps = [P.ps([128, 512]) for _ in range(2)]
        l = P.sb([128, NCH], F32)
        tmp = P.sb([128, NCH], F32)
        kk = [0]
        KMAX = int(os.environ.get('PRESTOP', 1000))

        def g(f, *a, **k):
            kk[0] += 1
            if kk[0] <= KMAX:
                f(*a, **k)
        for d in range(2):
            tri = triU if d == 0 else triL
            g(P.act, l[:, :NC_], G[:, :NC_, 2 * d + 1], AF.Exp, scale=-1.0, R=[G], W=[l])
            g(P.act, l[:, :NC_], l[:, :NC_], AF.Ln, bias=1.0, R=[l], W=[l])
            g(P.mm, gps[0][:, :NC_], tri, l[:, :NC_], R=[cst, l], W=[gps[0]])
            g(P.mm, gps[1][:, :NC_], ones_f[:], l[:, :NC_], R=[ones_f, l], W=[gps[1]])
            g(P.act, E[d][:, :NC_], gps[0][:, :NC_], AF.Exp, scale=-1.0, R=[gps[0]], W=[E[d]])
            g(P.act, Dc[d][:, :NC_], gps[1][:, :NC_], AF.Exp, scale=-1.0, R=[gps[1]], W=[Dc[d]])
            g(P.act, tmp[:, :NC_], gps[0][:, :NC_], AF.Exp, R=[gps[0]], W=[tmp])
            g(P.act, Wt[d][:, :NC_], G[:, :NC_, 2 * d], AF.Exp, bias=cx["lnscale"][:, 0:1], R=[G, cx["lnscale"]], W=[Wt[d]])
            g(P.v, "dve", "tensor_tensor", Wt[d][:, :NC_], Wt[d][:, :NC_], tmp[:, :NC_], ALU.mult, R=[Wt[d], tmp], W=[Wt[d]])
            g(P.v, "dve", "tensor_tensor", WD[d][:, :NC_], Wt[d][:, :NC_], Dc[d][:, :NC_], ALU.mult, R=[Wt[d], Dc[d]], W=[WD[d]])

    NC_ = NCsave
    Hs = P.sb([128, NCH, 256], F32)
    Hb = [Buf() for _ in range(NCH)]
    Cf = [P.sb([128, 257], F32) for _ in range(2)]
    Cb = [P.sb([128, 257], BF16) for _ in range(2)]
    for d in range(2):
        P.v("pool", "memset", Cf[d][:], 0.0, W=[Cf[d]])
        P.v("pool", "memset", Cb[d][:], 0.0, W=[Cb[d]])
    st_ps = [P.ps([128, 512]) for _ in range(2)]
    num_ps = [P.ps([128, 512]) for _ in range(2)]
    u_ps = [P.ps([128, 512]) for _ in range(2)]
    tr_ps = P.ps([128, 1024], BF16)
    pT = [P.sb([128, 128], BF16) for _ in range(2)]
    vw = [P.sb([128, 257], BF16) for _ in range(2)]
    sm = [[P.sb([128, 1], F32) for _ in range(3)] for _ in range(2)]
    junk = P.sb([128, 256], F32)
    ssq = P.sb([128, 1], F32)
    t1 = P.sb([128, 256], F32)
    gbl = [P.sb([128, 256], BF16) for _ in range(2)]
    yb = P.sb([128, 256], BF16)
    ybT = [P.sb([128, 2, 128], BF16) for _ in range(2)]
    done = [0] * NC_
    nfin = [0]

    def finalize(c):
        i2 = nfin[0] % 2
        nfin[0] += 1
        hm = Hs[:, c, :]
        P.dma("sp", gbl[i2][:], gb_d[c], R=[cx["gb_buf"][c]], W=[gbl[i2]])
        P.act(junk[:], hm, AF.Square, accum_out=ssq[:], R=[Hb[c]], W=[junk, ssq])
        P.act(ssq[:], ssq[:], AF.Sqrt, scale=1.0 / 256, bias=RMS_EPS, R=[ssq], W=[ssq])
        P.v("dve", "reciprocal", ssq[:], ssq[:], R=[ssq], W=[ssq])
        P.v("dve", "scalar_tensor_tensor", t1[:], hm, ssq[:, 0:1], hng[:], ALU.mult, ALU.mult, R=[Hb[c], ssq, hng], W=[t1])
        P.v("pool", "tensor_tensor", yb[:], t1[:], gbl[i2][:], ALU.mult, R=[t1, gbl[i2]], W=[yb])
        for vc in range(2):
            P.transpose(tr_ps[:, vc * 128:(vc + 1) * 128], yb[:, vc * 128:(vc + 1) * 128], ident_bf[:], R=[yb, ident_bf], W=[tr_ps])
        P.act(ybT[i2][:], tr_ps[:, 0:256].rearrange("p (a b) -> p a b", a=2), AF.Copy, R=[tr_ps], W=[ybT[i2]])
        P.dma("pool", yT[256:512, c * 128:(c + 1) * 128].rearrange("(a p) t -> p a t", p=128), ybT[i2][:], R=[ybT[i2]], final=True)

    import os
    for step in range(min(NC_, int(os.environ.get('CSTEPS', '1000')))):
        for d in range(2):
            c = step if d == 0 else int(os.environ.get('BWDTOP', NC_ - 1)) - step
            cs = slice(c * 128, (c + 1) * 128)
            mask = triU if d == 0 else triL
            s1, s2, s3 = sm[d]
            P.mm(st_ps[d][:, 0:128], kT[:, cs], qT[:, cs], R=[kT, qT], W=[st_ps[d]])
            P.v("dve", "scalar_tensor_tensor", pT[d][:], st_ps[d][:, 0:128], Wt[d][:, c:c + 1], mask, ALU.mult, ALU.mult,
                R=[st_ps[d], Wt[d], cst], W=[pT[d]])
            P.act(vw[d][:], v1[:, c, 0:257], AF.Copy, scale=WD[d][:, c:c + 1], R=[v1, WD[d]], W=[vw[d]])
            P.mm(num_ps[d][:, 0:257], pT[d][:], v1[:, c, 0:257], start=True, stop=False, R=[pT[d], v1], W=[num_ps[d]])
            P.mm(num_ps[d][:, 0:257], qT[:, cs], Cb[d][:], start=False, stop=True, R=[qT, Cb[d]], W=[num_ps[d]])
            P.mm(u_ps[d][:, 0:257], k_tm[:, c, :], vw[d][:], R=[k_tm, vw[d]], W=[u_ps[d]])
            P.v("dve", "scalar_tensor_tensor", Cf[d][:], Cf[d][:], Dc[d][:, c:c + 1], u_ps[d][:, 0:257], ALU.mult, ALU.add,
                R=[Cf[d], Dc[d], u_ps[d]], W=[Cf[d]])
            P.act(Cb[d][:], Cf[d][:], AF.Copy, R=[Cf[d]], W=[Cb[d]])
            P.v("dve", "tensor_tensor", s1[:], num_ps[d][:, 256:257], E[d][:, c:c + 1], ALU.mult, R=[num_ps[d], E[d]], W=[s1])
            P.v("dve", "tensor_scalar", s2[:], s1[:], -1.0, None, ALU.mult, R=[s1], W=[s2])
            P.v("dve", "scalar_tensor_tensor", s3[:], s1[:], 1.0, s2[:], ALU.max, ALU.max, R=[s1, s2], W=[s3])
            P.v("dve", "reciprocal", s3[:], s3[:], R=[s3], W=[s3])
            P.v("dve", "tensor_tensor", s3[:], s3[:], E[d][:, c:c + 1], ALU.mult, R=[E[d], s3], W=[s3])
            if done[c] == 0:
                P.act(Hs[:, c, :], num_ps[d][:, 0:256], AF.Copy, scale=s3[:, 0:1], R=[num_ps[d], s3], W=[Hb[c]])
                done[c] = 1
            else:
                P.v("dve", "scalar_tensor_tensor", Hs[:, c, :], num_ps[d][:, 0:256], s3[:, 0:1], Hs[:, c, :], ALU.mult, ALU.add,
                    R=[num_ps[d], s3, Hb[c]], W=[Hb[c]])
                finalize(c)


def build_p1_full(nc, ntiles=16):
    P, cx = build_p1(nc, ntiles)
    cx["gb_buf"] = [Buf() for _ in range(NCH)]
    lnscale = P.sb([128, 1], F32)
    P.v("pool", "memset", lnscale[:], DK_SCALE_LN, W=[lnscale])
    cx["lnscale"] = lnscale
    build_p1_mlstm(P, cx)
    P.finish()
    return nc


def consts_np():
    p = np.arange(128)
    ident = (p[:, None] == p[None, :]).astype(np.float32)
    triU = (p[:, None] <= p[None, :]).astype(np.float32)
    triL = (p[:, None] >= p[None, :]).astype(np.float32)
    return np.ascontiguousarray(np.stack([ident, triU, triL], axis=1))


def p1_inputs(inp, b, j):
    w = inp["even_w_in"][0]
    ch0 = 256 * j
    r = np.arange
    cols = np.concatenate([
        1024 + ch0 + r(128), ch0 + r(128), 1024 + ch0 + 128 + r(128), ch0 + 128 + r(128),
        2048 + ch0 + r(128), 2048 + ch0 + 128 + r(128),
        3072 + 128 * j + r(128), 3584 + 128 * j + r(128),
        3584 + 128 * j + r(128), 4096 + 256 * j + r(256), 7168 + 4 * r(4) + j,
        5120 + 256 * j + r(256), 6144 + 256 * j + r(256)])
    assert cols.shape[0] == 1924
    cpar = np.zeros((128, 2, 34), np.float32)
    for c in range(2):
        ch = ch0 + c * 128 + r(128)
        cpar[:, c, 0:31] = inp["even_conv_w"][0][:, ch].T
        cpar[:, c, 31] = inp["even_conv_b"][0][ch]
        cpar[:, c, 32] = inp["even_gn_g"][0][ch]
        cpar[:, c, 33] = inp["even_gn_b"][0][ch]
    return {
        "xT": np.ascontiguousarray(inp["x"][b].T),
        "w1": np.ascontiguousarray(w[:, cols]),
        "gpre": np.ascontiguousarray(inp["even_g_pre"][0].reshape(8, 128).T),
        "bg": np.ascontiguousarray(np.broadcast_to(inp["even_b_gate"][0][4 * r(4) + j][None, :], (128, 4))),
        "cpar": cpar,
        "hng": np.ascontiguousarray(np.broadcast_to(inp["even_hn_g"][0][256 * j:256 * j + 256][None, :], (128, 256))),
        "cst": consts_np(),
    }


TQ = 2048


def build_post(nc, KCI):
    P = Prog(nc)
    yin = dram_in(nc, "yin", [KCI * 128, TQ], BF16)
    xres = dram_in(nc, "xres", [1024, TQ])
    wout = dram_in(nc, "wout", [KCI * 128, 1024])
    gpost_d = dram_in(nc, "gpost", [128, 8])
    xo = dram_out(nc, "xo", [1024, TQ])
    gpost = P.sb([128, 8], F32)
    P.dma("sp", gpost[:], gpost_d, W=[gpost])
    ones_bf = P.sb([128, 128], BF16)
    P.v("pool", "memset", ones_bf[:], 1.0, W=[ones_bf])
    W = P.sb([128, KCI, 1024], BF16)
    with P.scope():
        stg = [P.sb([128, KCI, 256], F32) for _ in range(2)]
        load_weight_bf16(P, W, wout, 1024, gscale=None, KC=KCI, piece=256, stg=stg)
    yv = yin.rearrange("(kc p) t -> p kc t", p=128)
    xv = xres.rearrange("(kc p) t -> p kc t", p=128)
    ov = xo.rearrange("(kc p) t -> p kc t", p=128)
    yt = [P.sb([128, KCI, 512], BF16) for _ in range(2)]
    xt = [P.sb([128, 8, 512], F32) for _ in range(2)]
    yo = P.sb([128, 8, 512], F32)
    sq = P.sb([128, 8, 512], BF16)
    rstd = P.sb([128, 512], F32)
    tmp = [P.sb([128, 512], F32) for _ in range(2)]
    ot = P.sb([128, 8, 512], F32)
    ps = [P.ps([128, 512]) for _ in range(2)]
    ss_ps = P.ps([128, 512])
    n = 0
    for ti in range(TQ // 512):
        t0 = ti * 512
        y, x = yt[ti % 2], xt[ti % 2]
        P.dma("sp", y[:], yv[:, :, t0:t0 + 512], W=[y])
        P.dma("sp", x[:], xv[:, :, t0:t0 + 512], W=[x])
        for fc in range(8):
            p_ = ps[n % 2]
            n += 1
            for kc in range(KCI):
                P.mm(p_[:], W[:, kc, fc * 128:(fc + 1) * 128], y[:, kc, :], start=(kc == 0), stop=(kc == KCI - 1), R=[W, y], W=[p_])
            P.act(yo[:, fc, :], p_[:], AF.Copy, R=[p_], W=[yo])
        P.act(sq[:], yo[:], AF.Square, R=[yo], W=[sq])
        for kc in range(8):
            P.mm(ss_ps[:], ones_bf[:], sq[:, kc, :], start=(kc == 0), stop=(kc == 7), R=[sq, ones_bf], W=[ss_ps])
        P.act(rstd[:], ss_ps[:], AF.Sqrt, scale=1.0 / 1024, bias=RMS_EPS, R=[ss_ps], W=[rstd])
        P.v("dve", "reciprocal", rstd[:], rstd[:], R=[rstd], W=[rstd])
        for fc in range(8):
            e1 = "dve" if fc % 2 == 0 else "pool"
            tm_ = tmp[fc % 2]
            P.v(e1, "tensor_tensor", tm_[:], yo[:, fc, :], rstd[:], ALU.mult, R=[yo, rstd], W=[tm_])
            P.v("dve", "scalar_tensor_tensor", ot[:, fc, :], tm_[:], gpost[:, fc:fc + 1], x[:, fc, :], ALU.mult, ALU.add,
                R=[tm_, gpost, x], W=[ot])
        P.dma("pool", ov[:, :, t0:t0 + 512], ot[:], R=[ot], final=True)
    P.finish()
    return nc


NFM1 = 19


def build_pre1(nc):
    P = Prog(nc)
    xT = dram_in(nc, "xT", [1024, TQ])
    w = dram_in(nc, "w", [1024, NFM1 * 128 + 512])
    gpre_d = dram_in(nc, "gpre", [128, 8])
    lat = dram_out(nc, "lat", [7 * 128, TQ])
    sz = dram_out(nc, "sz", [12 * 128, TQ], BF16)
    fin = dram_out(nc, "fin", [TQ, 512])
    gpre = P.sb([128, 8], F32)
    P.dma("sp", gpre[:], gpre_d, W=[gpre])
    ones_bf = P.sb([128, 128], BF16)
    P.v("pool", "memset", ones_bf[:], 1.0, W=[ones_bf])
    NCOL = NFM1 * 128 + 512
    W = P.sb([128, 8, NCOL], BF16)
    with P.scope():
        stg = [P.sb([128, 8, 512], F32) for _ in range(2)]
        load_weight_bf16(P, W, w, NCOL, gscale=gpre, stg=stg)
    lt = [P.sb([128, 512], F32) for _ in range(2)]
    st = [P.sb([128, 512], BF16) for _ in range(2)]
    ft = [P.sb([128, 512], F32) for _ in range(2)]
    k = [0, 0, 0]

    def evac_fm(fc, ps, ti, t0):
        if fc < 7:
            t = lt[k[0] % 2]
            k[0] += 1
            P.act(t[:], ps[:], AF.Copy, R=[ps], W=[t])
            P.dma("pool", lat[fc * 128:(fc + 1) * 128, t0:t0 + 512], t[:], R=[t], final=True)
        else:
            t = st[k[1] % 2]
            k[1] += 1
            P.act(t[:], ps[:], AF.Silu, R=[ps], W=[t])
            P.dma("pool", sz[(fc - 7) * 128:(fc - 6) * 128, t0:t0 + 512], t[:], R=[t], final=True)

    def evac_tm(gi, ps, c):
        t = ft[k[2] % 2]
        k[2] += 1
        P.v("dve", "tensor_copy", t[:], ps[:], R=[ps], W=[t])
        P.dma("pool", fin[c * 128:(c + 1) * 128, :], t[:], R=[t], final=True)

    rms_proj_pass(P, xT, W, NFM1, 0, [(NFM1 * 128, 512)], evac_fm, evac_tm, ones_bf, TQ // 512)
    P.finish()
    return nc


def lat_norm(P, src, KC, T, dst, Dfeat, ones_bf):
    sv = src.rearrange("(kc p) t -> p kc t", p=128)
    with P.scope():
        xin = [P.sb([128, KC, 512], F32) for _ in range(2)]
        sq = P.sb([128, KC, 512], BF16)
        rstd = P.sb([128, 512], F32)
        ss = P.ps([128, 512])
        for ti in range(T // 512):
            t0 = ti * 512
            x = xin[ti % 2]
            P.dma("sp", x[:], sv[:, :, t0:t0 + 512], W=[x])
            P.act(sq[:], x[:], AF.Square, R=[x], W=[sq])
            for kc in range(KC):
                P.mm(ss[:], ones_bf[:], sq[:, kc, :], start=(kc == 0), stop=(kc == KC - 1), R=[sq, ones_bf], W=[ss])
            P.act(rstd[:], ss[:], AF.Sqrt, scale=1.0 / Dfeat, bias=RMS_EPS, R=[ss], W=[rstd])
            P.v("dve", "reciprocal", rstd[:], rstd[:], R=[rstd], W=[rstd])
            for kc in range(KC):
                P.v("dve" if kc % 2 == 0 else "pool", "tensor_tensor", dst[:, kc, t0:t0 + 512], x[:, kc, :], rstd[:], ALU.mult,
                    R=[x, rstd], W=[dst])


PI = float(np.pi)


def rope_apply(P, pos_d, T, rc, negpi, a_fn, b_fn, out_fn, nA):
    PT = 512
    with P.scope():
        pi_ = P.sb([128, PT], I32)
        ang = P.sb([128, PT], F32)
        kf = P.sb([128, PT], F32)
        ki = P.sb([128, PT], I32)
        r = P.sb([128, PT], F32)
        CT = P.sb([128, PT], F32)
        ST = P.sb([128, PT], F32)
        t1 = P.sb([128, PT], F32)
        t2 = P.sb([128, PT], F32)

        def sin_of(src_shift, dst):
            P.v("dve", "tensor_scalar", r[:], ang[:], src_shift, None, ALU.add, R=[ang], W=[r])
            P.v("dve", "tensor_scalar", kf[:], r[:], 1.0 / (2 * PI), None, ALU.mult, R=[r], W=[kf])
            P.v("dve", "tensor_copy", ki[:], kf[:], R=[kf], W=[ki])
            P.v("dve", "tensor_copy", kf[:], ki[:], R=[ki], W=[kf])
            P.v("dve", "scalar_tensor_tensor", r[:], kf[:], -2 * PI, r[:], ALU.mult, ALU.add, R=[kf, r], W=[r])
            P.v("dve", "tensor_scalar", r[:], r[:], PI, None, ALU.add, R=[r], W=[r])
            P.act(dst[:], r[:], AF.Sin, bias=negpi[:, 0:1], R=[r, negpi], W=[dst])

        for pc in range(T // PT):
            sl = slice(pc * PT, (pc + 1) * PT)
            P.dma("sp", pi_[:], pos_d[:, sl], W=[pi_])
            P.v("dve", "tensor_copy", ang[:], pi_[:], R=[pi_], W=[ang])
            P.v("dve", "tensor_scalar", ang[:], ang[:], rc[:, 0:1], None, ALU.mult, R=[ang, rc], W=[ang])
            sin_of(PI / 2, CT)
            sin_of(0.0, ST)
            P.v("dve", "tensor_scalar", ST[:], ST[:], rc[:, 1:2], None, ALU.mult, R=[ST, rc], W=[ST])
            for i in range(nA):
                a, ab = a_fn(i, sl)
                b, bb = b_fn(i, sl)
                o, ob = out_fn(i, sl)
                P.v("dve", "tensor_tensor", t1[:], a, CT[:], ALU.mult, R=[ab, CT], W=[t1])
                P.v("pool", "tensor_tensor", t2[:], b, ST[:], ALU.mult, R=[bb, ST], W=[t2])
                P.v("dve", "tensor_tensor", o, t1[:], t2[:], ALU.add, R=[t1, t2], W=[ob])


ATT_SCALE = float((128 + 64) ** -0.5)


def build_attn(nc, nheads=8):
    P = Prog(nc)
    cq = dram_in(nc, "cq", [384, TQ])
    ckv = dram_in(nc, "ckv", [256, S])
    kr = dram_in(nc, "kr", [256, S])
    posq = dram_in(nc, "posq", [128, TQ], I32)
    posk = dram_in(nc, "posk", [128, S], I32)
    rc_d = dram_in(nc, "rc", [128, 2])
    wuq = dram_in(nc, "wuq", [384, 2048])
    gq_d = dram_in(nc, "gq", [128, 3])
    wukv = dram_in(nc, "wukv", [256, 2048])
    gkv_d = dram_in(nc, "gkv", [128, 2])
    szc = dram_in(nc, "szc", [1024, TQ], BF16)
    yc = dram_out(nc, "yc", [1024, TQ], BF16)

    rc = P.sb([128, 2], F32)
    P.dma("sp", rc[:], rc_d, W=[rc])
    gq = P.sb([128, 3], F32)
    P.dma("sp", gq[:], gq_d, W=[gq])
    gkv = P.sb([128, 2], F32)
    P.dma("sp", gkv[:], gkv_d, W=[gkv])
    negpi = P.sb([128, 1], F32)
    P.v("pool", "memset", negpi[:], -PI, W=[negpi])
    ones_bf = P.sb([128, 128], BF16)
    P.v("pool", "memset", ones_bf[:], 1.0, W=[ones_bf])

    Wq = P.sb([128, 3, 2048], BF16)
    Wkv = P.sb([128, 2, 2048], BF16)
    with P.scope():
        stg = [P.sb([128, 3, 512], F32) for _ in range(2)]
        load_weight_bf16(P, Wq, wuq, 2048, gscale=gq, KC=3, stg=stg)
        load_weight_bf16(P, Wkv, wukv, 2048, gscale=gkv, KC=2, stg=stg)

    ckvn = P.sb([128, 2, S], BF16)
    lat_norm(P, ckv, 2, S, ckvn, 256, ones_bf)
    krope = P.sb([128, S], BF16)
    with P.scope():
        krs = P.sb([128, 2, S], F32)
        P.dma("sp", krs[:], kr.rearrange("(kc p) t -> p kc t", p=128), W=[krs])
        rope_apply(P, posk, S, rc, negpi,
                   lambda i, sl: (krs[:, 0, sl], krs), lambda i, sl: (krs[:, 1, sl], krs),
                   lambda i, sl: (krope[:, sl], krope), 1)

    cqn = P.sb([128, 3, TQ], BF16)
    lat_norm(P, cq, 3, TQ, cqn, 384, ones_bf)
    qn = P.sb([128, 8, TQ], BF16)
    qr = P.sb([128, 4, TQ], BF16)
    with P.scope():
        qab = P.sb([128, 8, TQ], BF16)
        pps = [P.ps([128, 512]) for _ in range(2)]
        n = 0
        for fc in range(16):
            for ti in range(TQ // 512):
                t0 = ti * 512
                p_ = pps[n % 2]
                n += 1
                for kc in range(3):
                    P.mm(p_[:], Wq[:, kc, fc * 128:(fc + 1) * 128], cqn[:, kc, t0:t0 + 512], start=(kc == 0), stop=(kc == 2),
                         R=[Wq, cqn], W=[p_])
                if fc < 8:
                    P.act(qn[:, fc, t0:t0 + 512], p_[:], AF.Copy, R=[p_], W=[qn])
                else:
                    P.act(qab[:, fc - 8, t0:t0 + 512], p_[:], AF.Copy, R=[p_], W=[qab])
        rope_apply(P, posq, TQ, rc, negpi,
                   lambda i, sl: (qab[:, i, sl], qab), lambda i, sl: (qab[:, 4 + i, sl], qab),
                   lambda i, sl: (qr[:, i, sl], qr), 4)

    KnT = P.sb([128, S], BF16)
    V = P.sb([128, 64, 128], BF16)
    kps = [P.ps([128, 512]) for _ in range(2)]
    sps = [P.ps([128, 512]) for _ in range(2)]
    ops_ = P.ps([128, 512])
    sums = P.ps([128, 512])
    pT = [P.sb([128, 512], BF16) for _ in range(3)]
    rec = P.sb([128, 512], F32)
    att = P.sb([128, 512], F32)
    zt = [P.sb([128, 512], BF16) for _ in range(2)]
    yt = [P.sb([128, 512], BF16) for _ in range(2)]
    nk = 0
    ns = 0
    npt = 0
    nz = 0
    for h in range(nheads):
        half = slice(0, 64) if h % 2 == 0 else slice(64, 128)
        for ti in range(S // 512):
            t0 = ti * 512
            p_ = kps[nk % 2]
            nk += 1
            for kc in range(2):
                P.mm(p_[:], Wkv[:, kc, h * 128:(h + 1) * 128], ckvn[:, kc, t0:t0 + 512], start=(kc == 0), stop=(kc == 1),
                     R=[Wkv, ckvn], W=[p_])
            P.act(KnT[:, t0:t0 + 512], p_[:], AF.Copy, R=[p_], W=[KnT])
        for c4 in range(16):
            p_ = kps[nk % 2]
            nk += 1
            for s4 in range(4):
                c = c4 * 4 + s4
                for kc in range(2):
                    P.mm(p_[:, s4 * 128:(s4 + 1) * 128], ckvn[:, kc, c * 128:(c + 1) * 128], Wkv[:, kc, 1024 + h * 128:1024 + (h + 1) * 128],
                         start=(kc == 0), stop=(kc == 1), R=[Wkv, ckvn], W=[p_])
            P.v("dve", "tensor_copy", V[:, c4 * 4:(c4 + 1) * 4, :], p_[:].rearrange("p (a b) -> p a b", a=4), R=[p_], W=[V])
        for ti in range(TQ // 512):
            t0 = ti * 512
            for c in range(64):
                sp_ = sps[ns % 2]
                ns += 1
                cs = slice(c * 128, (c + 1) * 128)
                P.mm(sp_[:], KnT[:, cs], qn[:, h, t0:t0 + 512], start=True, stop=False, R=[KnT, qn], W=[sp_])
                P.mm(sp_[:], krope[half, cs], qr[half, h // 2, t0:t0 + 512], start=False, stop=True, R=[krope, qr], W=[sp_])
                pt = pT[npt % 3]
                npt += 1
                P.act(pt[:], sp_[:], AF.Exp, scale=ATT_SCALE, R=[sp_], W=[pt])
                P.mm(ops_[:], V[:, c, :], pt[:], start=(c == 0), stop=(c == 63), R=[V, pt], W=[ops_])
                P.mm(sums[:], ones_bf[:], pt[:], start=(c == 0), stop=(c == 63), R=[ones_bf, pt], W=[sums])
            z = zt[nz % 2]
            y = yt[nz % 2]
            nz += 1
            P.dma("sp", z[:], szc[h * 128:(h + 1) * 128, t0:t0 + 512], W=[z])
            P.v("dve", "reciprocal", rec[:], sums[:], R=[sums], W=[rec])
            P.v("dve", "tensor_tensor", att[:], ops_[:], rec[:], ALU.mult, R=[ops_, rec], W=[att])
            P.v("pool", "tensor_tensor", y[:], att[:], z[:], ALU.mult, R=[att, z], W=[y])
            P.dma("pool", yc[h * 128:(h + 1) * 128, t0:t0 + 512], y[:], R=[y], final=True)
    P.finish()
    return nc


def build_dft(nc):
    P = Prog(nc)
    X = dram_in(nc, "X", [S, 128])
    wcs_d = dram_in(nc, "wcs", [128, 256])
    tw_d = dram_in(nc, "tw", [128, 2, 64 * 128])
    ccs_d = dram_in(nc, "ccs", [128, 2, 128])
    w64_d = dram_in(nc, "w64", [128, 64])
    frT = dram_out(nc, "frT", [128, S])

    L1 = P.sb([128, 128, 2, 64], BF16)
    L2 = P.sb([128, 128, 2, 64], BF16)
    ccs = P.sb([128, 2, 128], BF16)
    w64 = P.sb([128, 64], BF16)
    with P.scope():
        Xb = P.sb([128, 64 * 128], BF16)
        wcs = P.sb([128, 256], BF16)
        with P.scope():
            Xs = P.sb([128, 64 * 128], F32)
            P.dma("sp", Xs[:], X.rearrange("(n1 n2) c -> n1 (n2 c)", n2=64), W=[Xs])
            P.v("dve", "tensor_copy", Xb[:, 0:4096], Xs[:, 0:4096], R=[Xs], W=[Xb])
            P.v("pool", "tensor_copy", Xb[:, 4096:8192], Xs[:, 4096:8192], R=[Xs], W=[Xb])
            c1 = P.sb([128, 256], F32)
            P.dma("sp", c1[:], wcs_d, W=[c1])
            P.v("dve", "tensor_copy", wcs[:], c1[:], R=[c1], W=[wcs])
            c2 = P.sb([128, 2, 128], F32)
            P.dma("sp", c2[:], ccs_d, W=[c2])
            P.v("dve", "tensor_copy", ccs[:], c2[:], R=[c2], W=[ccs])
            c3 = P.sb([128, 64], F32)
            P.dma("sp", c3[:], w64_d, W=[c3])
            P.v("dve", "tensor_copy", w64[:], c3[:], R=[c3], W=[w64])
        TW = P.sb([128, 2, 64 * 128], F32)
        P.dma("sp", TW[:], tw_d, W=[TW])
        NB = 8
        Pm = P.sb([128, NB, 256], F32)
        M = [P.sb([128, NB, 128], F32) for _ in range(4)]
        pp = [P.ps([128, 512]) for _ in range(2)]
        n = 0
        for blk in range(64 // NB):
            for i2 in range(NB // 2):
                p_ = pp[n % 2]
                n += 1
                for j in range(2):
                    n2 = blk * NB + i2 * 2 + j
                    P.mm(p_[:, j * 256:(j + 1) * 256], Xb[:, n2 * 128:(n2 + 1) * 128], wcs[:], R=[Xb, wcs], W=[p_])
                P.act(Pm[:, i2 * 2:i2 * 2 + 2, :], p_[:].rearrange("p (a b) -> p a b", a=2), AF.Copy, R=[p_], W=[Pm])
            P1 = Pm[:, :, 0:128]
            P2 = Pm[:, :, 128:256]
            tc = TW[:, 0, blk * NB * 128:(blk + 1) * NB * 128].rearrange("p (a b) -> p a b", a=NB)
            ts = TW[:, 1, blk * NB * 128:(blk + 1) * NB * 128].rearrange("p (a b) -> p a b", a=NB)
            P.v("dve", "tensor_tensor", M[0][:], P1, tc, ALU.mult, R=[Pm, TW], W=[M[0]])
            P.v("pool", "tensor_tensor", M[1][:], P2, ts, ALU.mult, R=[Pm, TW], W=[M[1]])
            P.v("dve", "tensor_tensor", M[2][:], P2, tc, ALU.mult, R=[Pm, TW], W=[M[2]])
            P.v("pool", "tensor_tensor", M[3][:], P1, ts, ALU.mult, R=[Pm, TW], W=[M[3]])
            nsl = slice(blk * NB, (blk + 1) * NB)

            def lv(L, hf):
                return L[:, :, hf, nsl].rearrange("p k n -> p n k")
            P.v("dve", "tensor_tensor", lv(L1, 0), M[0][:], M[1][:], ALU.subtract, R=[M[0], M[1]], W=[L1])
            P.v("pool", "tensor_tensor", lv(L2, 1), M[1][:], M[0][:], ALU.subtract, R=[M[0], M[1]], W=[L2])
            P.v("dve", "scalar_tensor_tensor", lv(L1, 1), M[2][:], -1.0, M[3][:], ALU.mult, ALU.subtract,
                R=[M[2], M[3]], W=[L1])
            P.v("dve", "scalar_tensor_tensor", lv(L2, 0), M[2][:], -1.0, M[3][:], ALU.mult, ALU.subtract,
                R=[M[2], M[3]], W=[L2])
    Es = P.sb([128, 128, 128], BF16)
    YT = P.sb([128, S], F32)
    ep = [P.ps([128, 512]) for _ in range(2)]
    yp = [P.ps([128, 512]) for _ in range(2)]
    n = 0
    for k4 in range(32):
        p_ = ep[n % 2]
        n += 1
        for j in range(4):
            k1 = k4 * 4 + j
            P.mm(p_[:, j * 128:(j + 1) * 128], L1[:, k1, :, :].rearrange("p a b -> p (a b)"), ccs[:, 0, :], start=True, stop=False,
                 R=[L1, ccs], W=[p_])
            P.mm(p_[:, j * 128:(j + 1) * 128], L2[:, k1, :, :].rearrange("p a b -> p (a b)"), ccs[:, 1, :], start=False, stop=True,
                 R=[L2, ccs], W=[p_])
        P.act(Es[:, k4 * 4:(k4 + 1) * 4, :], p_[:].rearrange("p (a b) -> p a b", a=4), AF.Copy, R=[p_], W=[Es])
    n = 0
    YTv = YT[:].rearrange("p (k2 k1) -> p k1 k2", k1=128)
    for k8 in range(16):
        p_ = yp[n % 2]
        n += 1
        for j in range(8):
            k1 = k8 * 8 + j
            P.mm(p_[:, j * 64:(j + 1) * 64], Es[:, k1, :], w64[:], R=[Es, w64], W=[p_])
        P.v("dve", "tensor_copy", YTv[:, k8 * 8:(k8 + 1) * 8, :], p_[:].rearrange("p (a b) -> p a b", a=8), R=[p_], W=[YT])
    for q in range(4):
        P.dma("pool", frT[:, q * 2048:(q + 1) * 2048], YT[:, q * 2048:(q + 1) * 2048], R=[YT], final=True)
    P.finish()
    return nc


def build_yd(nc):
    P = Prog(nc)
    fr = dram_in(nc, "fr", [512, TQ])
    szd = dram_in(nc, "szd", [512, TQ], BF16)
    wfd = dram_in(nc, "wfd", [512, 512])
    yd = dram_out(nc, "yd", [512, TQ], BF16)
    W = P.sb([128, 4, 512], BF16)
    with P.scope():
        stg = [P.sb([128, 4, 512], F32) for _ in range(2)]
        load_weight_bf16(P, W, wfd, 512, gscale=None, KC=4, stg=stg)
    fv = fr.rearrange("(kc p) t -> p kc t", p=128)
    zv = szd.rearrange("(kc p) t -> p kc t", p=128)
    ov = yd.rearrange("(kc p) t -> p kc t", p=128)
    ft = [P.sb([128, 4, 512], F32) for _ in range(2)]
    fb = [P.sb([128, 4, 512], BF16) for _ in range(2)]
    zt = [P.sb([128, 4, 512], BF16) for _ in range(2)]
    ot = [P.sb([128, 4, 512], BF16) for _ in range(2)]
    ps = [P.ps([128, 512]) for _ in range(2)]
    n = 0
    for ti in range(TQ // 512):
        t0 = ti * 512
        i2 = ti % 2
        P.dma("sp", ft[i2][:], fv[:, :, t0:t0 + 512], W=[ft[i2]])
        P.dma("sp", zt[i2][:], zv[:, :, t0:t0 + 512], W=[zt[i2]])
        P.v("dve", "tensor_copy", fb[i2][:], ft[i2][:], R=[ft[i2]], W=[fb[i2]])
        for fc in range(4):
            p_ = ps[n % 2]
            n += 1
            for kc in range(4):
                P.mm(p_[:], W[:, kc, fc * 128:(fc + 1) * 128], fb[i2][:, kc, :], start=(kc == 0), stop=(kc == 3), R=[W, fb[i2]], W=[p_])
            P.v("dve", "tensor_tensor", ot[i2][:, fc, :], p_[:], zt[i2][:, fc, :], ALU.mult, R=[p_, zt[i2]], W=[ot[i2]])
        P.dma("pool", ov[:, :, t0:t0 + 512], ot[i2][:], R=[ot[i2]], final=True)
    P.finish()
    return nc


def _run(build, in_maps):
    nc = bass.Bass("TRN2", target_bir_lowering=False)
    build(nc)
    res = run_bass_kernel_spmd(nc, in_maps, core_ids=list(range(8)))
    return res.results


def _cg(v, k):
    return np.ascontiguousarray(v.reshape(k, 128).T)


def kernel(**inp):
    inp = {k: np.asarray(v) for k, v in inp.items()}
    B = 2
    r = np.arange
    res = _run(build_p1_full, [p1_inputs(inp, c // 4, c % 4) for c in range(8)])
    y0T = []
    for b in range(B):
        y = np.empty((2048, S), dtype=res[0]["yT"].dtype)
        for j in range(4):
            yt = np.asarray(res[b * 4 + j]["yT"])
            y[256 * j:256 * j + 256] = yt[0:256]
            y[1024 + 256 * j:1024 + 256 * j + 256] = yt[256:512]
        y0T.append(y)
    xT = [np.ascontiguousarray(inp["x"][b].T) for b in range(B)]
    sl = [slice(TQ * q, TQ * (q + 1)) for q in range(4)]
    res = _run(lambda nc: build_post(nc, 16), [
        {"yin": np.ascontiguousarray(y0T[c // 4][:, sl[c % 4]]), "xres": np.ascontiguousarray(xT[c // 4][:, sl[c % 4]]),
         "wout": np.ascontiguousarray(inp["even_w_out"][0]), "gpost": _cg(inp["even_g_post"][0], 8)} for c in range(8)])
    x1T = [np.asarray(res[c]["xo"]) for c in range(8)]
    ksw = 640 + np.concatenate([32 + r(32), r(32)])
    cols = np.concatenate([r(640), 640 + r(64), 640 + r(64), ksw, ksw, 1216 + r(1024), 2240 + r(512), 704 + r(512)])
    w1 = np.ascontiguousarray(inp["odd_w_in"][0][:, cols])
    res = _run(build_pre1, [{"xT": x1T[c], "w": w1, "gpre": _cg(inp["odd_g_pre"][0], 8)} for c in range(8)])
    lat = [np.asarray(res[c]["lat"]) for c in range(8)]
    sz = [np.asarray(res[c]["sz"]) for c in range(8)]
    fin = [np.asarray(res[c]["fin"]) for c in range(8)]
    inv = (np.float32(10000.0) ** (-np.arange(0, 64, 2, dtype=np.float32) / np.float32(64))).astype(np.float32)
    p = r(128)
    rc = np.stack([inv[p % 32], np.where((p % 64) < 32, -1.0, 1.0).astype(np.float32)], axis=1).astype(np.float32)
    nope = np.concatenate([192 * h + r(128) for h in range(8)])
    swp = np.concatenate([32 + r(32), r(32)])
    A = np.concatenate([192 * h + 128 + r(64) for h in range(8)])
    Bc = np.concatenate([192 * h + 128 + swp for h in range(8)])
    wuq = np.ascontiguousarray(inp["odd_w_uq"][0][:, np.concatenate([nope, A, Bc])])
    kcol = np.concatenate([256 * h + r(128) for h in range(8)])
    vcol = np.concatenate([256 * h + 128 + r(128) for h in range(8)])
    wukv = np.ascontiguousarray(inp["odd_w_ukv"][0][:, np.concatenate([kcol, vcol])])
    maps = []
    for c in range(8):
        b, q = c // 4, c % 4
        ckv = np.ascontiguousarray(np.concatenate([lat[b * 4 + qq][384:640] for qq in range(4)], axis=1))
        kr = np.ascontiguousarray(np.concatenate([lat[b * 4 + qq][640:896] for qq in range(4)], axis=1))
        pos = inp["positions"][b].astype(np.int32)
        maps.append({"cq": np.ascontiguousarray(lat[c][0:384]), "ckv": ckv, "kr": kr,
                     "posq": np.ascontiguousarray(np.broadcast_to(pos[None, sl[q]], (128, TQ))),
                     "posk": np.ascontiguousarray(np.broadcast_to(pos[None, :], (128, S))),
                     "rc": rc, "wuq": wuq, "gq": _cg(inp["odd_g_q"][0], 3), "wukv": wukv, "gkv": _cg(inp["odd_g_kv"][0], 2),
                     "szc": np.ascontiguousarray(sz[c][0:1024])})
    res = _run(build_attn, maps)
    yc = [np.asarray(res[c]["yc"]) for c in range(8)]
    n1 = r(128)
    a128 = 2 * np.pi * ((n1[:, None] * n1[None, :]) % 128) / 128.0
    wcs = np.concatenate([np.cos(a128), np.sin(a128)], axis=1).astype(np.float32)
    n2 = r(64)
    atw = 2 * np.pi * (n2[:, None] * n1[None, :]) / 8192.0
    tw = np.stack([np.cos(atw).reshape(-1), np.sin(atw).reshape(-1)], axis=0).astype(np.float32)
    tw = np.ascontiguousarray(np.broadcast_to(tw[None], (128, 2, 8192)))
    ccs = np.ascontiguousarray(np.stack([np.cos(a128), np.sin(a128)], axis=1).astype(np.float32))
    a64 = 2 * np.pi * ((n2[:, None] * n2[None, :]) % 64) / 64.0
    w64 = (np.concatenate([np.cos(a64), np.sin(a64)], axis=0) / 1024.0).astype(np.float32)
    maps = []
    for c in range(8):
        b, g = c // 4, c % 4
        X = np.ascontiguousarray(np.concatenate([fin[b * 4 + qq][:, 128 * g:128 * g + 128] for qq in range(4)], axis=0))
        maps.append({"X": X, "wcs": wcs, "tw": tw, "ccs": ccs, "w64": w64})
    res = _run(build_dft, maps)
    frT = [np.asarray(res[c]["frT"]) for c in range(8)]
    maps = []
    for c in range(8):
        b, q = c // 4, c % 4
        fr = np.ascontiguousarray(np.concatenate([frT[b * 4 + g][:, sl[q]] for g in range(4)], axis=0))
        maps.append({"fr": fr, "szd": np.ascontiguousarray(sz[c][1024:1536]), "wfd": np.ascontiguousarray(inp["odd_w_fd"][0])})
    res = _run(build_yd, maps)
    yd = [np.asarray(res[c]["yd"]) for c in range(8)]
    res = _run(lambda nc: build_post(nc, 12), [
        {"yin": np.ascontiguousarray(np.concatenate([yc[c], yd[c]], axis=0)), "xres": x1T[c],
         "wout": np.ascontiguousarray(inp["odd_w_out"][0]), "gpost": _cg(inp["odd_g_post"][0], 8)} for c in range(8)])
    out = np.empty((B, S, 1024), np.float32)
    for c in range(8):
        out[c // 4, sl[c % 4], :] = np.asarray(res[c]["xo"]).T
    return out
```
